# Optimizing a Trainium2 kernel written in Bass

```python
import math
import jax, jax.numpy as jnp
from jax import lax
import numpy as np

D_MODEL = 2048
BATCH = 4
SEQ = 4096
DEPTH = 1

GRID_W = 64
NA_HEADS = 16
NA_HEAD_DIM = 64
D_NA = NA_HEADS * NA_HEAD_DIM
D_HY = D_MODEL - D_NA
NA_WIN_ROWS_MAX = 8
NA_WIN_COLS = 16
SHORT_CONV_W = 3
FILTER_EMB = 33
FILTER_ORDER = 64
DECAY_TARGET = 1e-2
FAST_DECAY_PCT = 0.3
SLOW_DECAY_PCT = 1.5
D_FF = 5632
FFN_CONV_W = 3
D_IN = 3 * D_NA + 3 * D_HY
N_MOD = 6
EPS = 1e-6

kernel_name = 'hymba_na_hyena_convglu_block'


def rmsnorm(x, g):
    xf = x.astype(jnp.float32)
    y = xf * lax.rsqrt(jnp.mean(xf * xf, axis=-1, keepdims=True) + EPS)
    return (y * g.astype(jnp.float32)).astype(x.dtype)


def dwconv_centred(u, w, b):
    K = w.shape[0]
    pad = K // 2
    L = u.shape[1]
    up = jnp.pad(u, ((0, 0), (pad, pad), (0, 0)))
    out = up[:, 0:L] * w[0]
    for j in range(1, K):
        out = out + up[:, j:j + L] * w[j]
    return out + b


def neighbourhood_attention(q, k, v, rpb):
    B, L, H, Dh = q.shape
    rows = L // GRID_W
    kh = min(NA_WIN_ROWS_MAX, rows)
    q = q.reshape(B, rows, GRID_W, H, Dh)
    k = k.reshape(B, rows, GRID_W, H, Dh)
    v = v.reshape(B, rows, GRID_W, H, Dh)
    r = jnp.arange(rows)
    row_start = jnp.clip(r - kh // 2, 0, rows - kh)
    key_rows = row_start[:, None] + jnp.arange(kh)[None, :]
    kb = k[:, key_rows]
    vb = v[:, key_rows]
    s = jnp.einsum('brqhd,brikhd->bhrqik', q, kb).astype(jnp.float32) * (NA_HEAD_DIM ** -0.5)
    cq = jnp.arange(GRID_W)
    ck = jnp.arange(GRID_W)
    col_start = jnp.clip(cq - NA_WIN_COLS // 2, 0, GRID_W - NA_WIN_COLS)
    col_mask = (ck[None, :] >= col_start[:, None]) & (ck[None, :] < col_start[:, None] + NA_WIN_COLS)
    row_off = key_rows - r[:, None] + (NA_WIN_ROWS_MAX - 1)
    col_off = jnp.clip(ck[None, :] - cq[:, None], -(NA_WIN_COLS - 1), NA_WIN_COLS - 1) + (NA_WIN_COLS - 1)
    bias = rpb[:, row_off[:, None, :, None], col_off[None, :, None, :]]
    s = s + bias.astype(jnp.float32)[None]
    s = jnp.where(col_mask[:, None, :], s, -jnp.inf)
    p = jax.nn.softmax(s.reshape(B, H, rows, GRID_W, kh * GRID_W), axis=-1)
    p = p.reshape(B, H, rows, GRID_W, kh, GRID_W).astype(vb.dtype)
    o = jnp.einsum('bhrqik,brikhd->brqhd', p, vb)
    return o.reshape(B, L, H * Dh)


def hyena_filters(L, w1, b1, w2, b2, w3, b3, w4, freq):
    f32 = jnp.float32
    t = jnp.linspace(0.0, 1.0, L, dtype=f32)[:, None]
    bands = (FILTER_EMB - 1) // 2
    w = 2.0 * math.pi * jnp.arange(L, dtype=f32)[:, None] / L
    fr = jnp.linspace(1e-4, bands - 1, bands, dtype=f32)[None, :]
    z = jnp.concatenate([t, jnp.cos(fr * w), -jnp.sin(fr * w)], axis=-1)
    fq = freq.astype(f32)
    h = jnp.sin(fq * (z @ w1.astype(f32) + b1.astype(f32)))
    h = jnp.sin(fq * (h @ w2.astype(f32) + b2.astype(f32)))
    h = jnp.sin(fq * (h @ w3.astype(f32) + b3.astype(f32)))
    h = (h @ w4.astype(f32)).reshape(L, 2, D_HY)
    max_decay = math.log(DECAY_TARGET) / FAST_DECAY_PCT
    min_decay = math.log(DECAY_TARGET) / SLOW_DECAY_PCT
    deltas = jnp.linspace(min_decay, max_decay, D_HY, dtype=f32)[None, :]
    decay = jnp.exp(-t * jnp.abs(deltas))
    h = h * decay[:, None, :]
    return h[:, 0], h[:, 1]


def bidirectional_fftconv(u, h_fwd, h_bwd):
    L = u.shape[1]
    C = u.shape[2]
    two_sided = jnp.concatenate([h_fwd, jnp.zeros((1, C), jnp.float32), h_bwd[:0:-1]], axis=0)
    k_f = jnp.fft.rfft(two_sided, n=2 * L, axis=0)
    u_f = jnp.fft.rfft(u.astype(jnp.float32), n=2 * L, axis=1)
    y = jnp.fft.irfft(u_f * k_f[None], n=2 * L, axis=1)[:, :L]
    return y.astype(u.dtype)


def hyena_mixer(hy_in, short_w, short_b, w1, b1, w2, b2, w3, b3, w4, freq, d_bias):
    L = hy_in.shape[1]
    uc = dwconv_centred(hy_in, short_w, short_b)
    x0, x1, v = jnp.split(uc, 3, axis=-1)
    h_fwd, h_bwd = hyena_filters(L, w1, b1, w2, b2, w3, b3, w4, freq)
    v = v * x1
    v = bidirectional_fftconv(v, h_fwd, h_bwd) + v * d_bias
    return v * x0


def setup_inputs(seed: int = 0) -> dict:
    key = jax.random.key(seed)
    ks = jax.random.split(key, 32)
    f32 = jnp.float32

    def nrm(k, shape, scale):
        return jax.random.normal(k, shape, f32) * scale

    Lr = DEPTH
    return {
        'x': nrm(ks[0], (BATCH, SEQ, D_MODEL), 1.0),
        'c': nrm(ks[1], (BATCH, D_MODEL), 1.0),
        'w_ada': nrm(ks[2], (Lr, D_MODEL, N_MOD * D_MODEL), 0.5 * D_MODEL ** -0.5),
        'b_ada': nrm(ks[3], (Lr, N_MOD * D_MODEL), 0.02),
        'g_mix': 1.0 + nrm(ks[4], (Lr, D_MODEL), 0.02),
        'w_in': nrm(ks[5], (Lr, D_MODEL, D_IN), D_MODEL ** -0.5),
        'na_rpb': nrm(ks[6], (Lr, NA_HEADS, 2 * NA_WIN_ROWS_MAX - 1, 2 * NA_WIN_COLS - 1), 0.1),
        'hy_short_w': nrm(ks[7], (Lr, SHORT_CONV_W, 3 * D_HY), SHORT_CONV_W ** -0.5),
        'hy_short_b': nrm(ks[8], (Lr, 3 * D_HY), 0.02),
        'hy_filt_w1': nrm(ks[9], (Lr, FILTER_EMB, FILTER_ORDER), FILTER_EMB ** -0.5),
        'hy_filt_b1': nrm(ks[10], (Lr, FILTER_ORDER), 0.02),
        'hy_filt_w2': nrm(ks[11], (Lr, FILTER_ORDER, FILTER_ORDER), FILTER_ORDER ** -0.5),
        'hy_filt_b2': nrm(ks[12], (Lr, FILTER_ORDER), 0.02),
        'hy_filt_w3': nrm(ks[13], (Lr, FILTER_ORDER, FILTER_ORDER), FILTER_ORDER ** -0.5),
        'hy_filt_b3': nrm(ks[14], (Lr, FILTER_ORDER), 0.02),
        'hy_filt_w4': nrm(ks[15], (Lr, FILTER_ORDER, 2 * D_HY), FILTER_ORDER ** -0.5),
        'hy_filt_freq': 1.0 + nrm(ks[16], (Lr, FILTER_ORDER), 0.01),
        'hy_bias': nrm(ks[17], (Lr, D_HY), 0.1),
        'beta_na': 1.0 + nrm(ks[18], (Lr, D_NA), 0.02),
        'beta_hy': 1.0 + nrm(ks[19], (Lr, D_HY), 0.02),
        'w_out': nrm(ks[20], (Lr, D_MODEL, D_MODEL), D_MODEL ** -0.5),
        'g_ffn': 1.0 + nrm(ks[21], (Lr, D_MODEL), 0.02),
        'w_up': nrm(ks[22], (Lr, D_MODEL, 2 * D_FF), D_MODEL ** -0.5),
        'ffn_conv_w': nrm(ks[23], (Lr, FFN_CONV_W, D_FF), FFN_CONV_W ** -0.5),
        'ffn_conv_b': nrm(ks[24], (Lr, D_FF), 0.02),
        'w_down': nrm(ks[25], (Lr, D_FF, D_MODEL), D_FF ** -0.5),
        'g_final': 1.0 + nrm(ks[26], (D_MODEL,), 0.02),
    }


def reference(x, c, w_ada, b_ada, g_mix, w_in, na_rpb, hy_short_w, hy_short_b,
              hy_filt_w1, hy_filt_b1, hy_filt_w2, hy_filt_b2, hy_filt_w3, hy_filt_b3,
              hy_filt_w4, hy_filt_freq, hy_bias, beta_na, beta_hy, w_out, g_ffn,
              w_up, ffn_conv_w, ffn_conv_b, w_down, g_final):
    B, L, _ = x.shape
    cond = jax.nn.silu(c)
    for l in range(DEPTH):
        mod = cond @ w_ada[l] + b_ada[l]
        sh1, sc1, gt1, sh2, sc2, gt2 = jnp.split(mod[:, None, :], N_MOD, axis=-1)
        h = rmsnorm(x, g_mix[l]) * (1.0 + sc1) + sh1
        proj = h @ w_in[l]
        q = proj[..., 0:D_NA].reshape(B, L, NA_HEADS, NA_HEAD_DIM)
        k = proj[..., D_NA:2 * D_NA].reshape(B, L, NA_HEADS, NA_HEAD_DIM)
        v = proj[..., 2 * D_NA:3 * D_NA].reshape(B, L, NA_HEADS, NA_HEAD_DIM)
        hy_in = proj[..., 3 * D_NA:]
        na_out = neighbourhood_attention(q, k, v, na_rpb[l])
        hy_out = hyena_mixer(hy_in, hy_short_w[l], hy_short_b[l],
                             hy_filt_w1[l], hy_filt_b1[l], hy_filt_w2[l], hy_filt_b2[l],
                             hy_filt_w3[l], hy_filt_b3[l], hy_filt_w4[l], hy_filt_freq[l],
                             hy_bias[l])
        merged = jnp.concatenate([rmsnorm(na_out, beta_na[l]), rmsnorm(hy_out, beta_hy[l])], axis=-1)
        x = x + gt1 * (merged @ w_out[l])
        h = rmsnorm(x, g_ffn[l]) * (1.0 + sc2) + sh2
        a, b = jnp.split(h @ w_up[l], 2, axis=-1)
        a = dwconv_centred(a, ffn_conv_w[l], ffn_conv_b[l])
        x = x + gt2 * ((jax.nn.gelu(a, approximate=False) * b) @ w_down[l])
    return rmsnorm(x, g_final)
```

```python
import numpy as np
from contextlib import ExitStack
import concourse.bass as bass
import concourse.mybir as mybir
from concourse.bass_utils import run_bass_kernel_spmd

F32 = mybir.dt.float32
BF16 = mybir.dt.bfloat16
AF = mybir.ActivationFunctionType
ALU = mybir.AluOpType

P = 128
D = 2048
KC = 16
L = 4096
NB = 32
DNA = 1024
DHY = 1024
DFF = 5632
NFC = 44
EPS = 1e-6
ZL = 8192
PI = float(np.pi)


class Tok:
    __slots__ = ("sem", "val")

    def __init__(self, sem, val):
        self.sem = sem
        self.val = val


class Ctx:
    def __init__(self, nc):
        self.nc = nc
        self.engs = {}
        for name, e in (("pe", nc.tensor), ("act", nc.scalar), ("dve", nc.vector),
                        ("pool", nc.gpsimd), ("sp", nc.sync)):
            self.engs[name] = dict(e=e, sem=nc.alloc_semaphore("sem_" + name), cnt=0, seen={}, pending=[])
        self.lastw = {}
        self.readers = {}
        self.dsem = {}

    def _deps(self, reads, writes):
        d = []
        for k in list(reads) + list(writes):
            t = self.lastw.get(k)
            if t is not None:
                d.append(t)
        for k in writes:
            d.extend(self.readers.get(k, {}).values())
        return d

    def _wait(self, en, deps):
        E = self.engs[en]
        for t in deps:
            if t.sem is E["sem"] and en == "pe":
                continue
            assert t.val is not None, "unresolved dependency"
            key = id(t.sem)
            if E["seen"].get(key, 0) >= t.val:
                continue
            E["e"].wait_ge(t.sem, t.val)
            E["seen"][key] = t.val

    def _record(self, tok, reads, writes):
        for k in writes:
            self.lastw[k] = tok
            self.readers[k] = {}
        for k in reads:
            self.readers.setdefault(k, {})[id(tok.sem)] = tok

    def op(self, en, fn, reads=(), writes=(), signal=True):
        E = self.engs[en]
        self._wait(en, self._deps(reads, writes))
        ins = fn(E["e"])
        if signal:
            E["cnt"] += 1
            ins.then_inc(E["sem"], 1)
            tok = Tok(E["sem"], E["cnt"])
            for p in E["pending"]:
                p.val = E["cnt"]
            E["pending"] = []
        else:
            tok = Tok(E["sem"], None)
            E["pending"].append(tok)
        self._record(tok, reads, writes)
        return tok

    def dma(self, q, ch, out, in_, reads=(), writes=(), **kw):
        E = self.engs[q]
        self._wait(q, self._deps(reads, writes))
        if ch not in self.dsem:
            self.dsem[ch] = [self.nc.alloc_semaphore("d_" + str(ch)), 0]
        d = self.dsem[ch]
        d[1] += 16
        E["e"].dma_start(out=out, in_=in_, **kw).then_inc(d[0], 16)
        tok = Tok(d[0], d[1])
        self._record(tok, reads, writes)
        return tok

    def barrier(self):
        assert not self.engs["pe"]["pending"], "PE group left unsignaled before barrier"
        toks = []
        for X in self.engs.values():
            if X["cnt"] > 0:
                toks.append(Tok(X["sem"], X["cnt"]))
        for d in self.dsem.values():
            if d[1] > 0:
                toks.append(Tok(d[0], d[1]))
        for n in self.engs:
            self._wait(n, toks)

    def finish(self):
        E = self.engs["sp"]
        for d in self.dsem.values():
            if d[1] > 0:
                E["e"].wait_ge(d[0], d[1])
        for n in ("pe", "act", "dve", "pool"):
            X = self.engs[n]
            if X["cnt"] > 0:
                E["e"].wait_ge(X["sem"], X["cnt"])


DEBUG = False


def build_nc():
    nc = bass.Bass("TRN2", target_bir_lowering=False)

    def din(name, shape, dt=F32):
        return nc.dram_tensor(name, list(shape), dt, kind="ExternalInput").ap()

    def dscr(name, shape, dt):
        return nc.dram_tensor(name, list(shape), dt, kind=("ExternalOutput" if DEBUG else "Internal")).ap()

    x = din("x", [L, D])
    c = din("c", [1, D])
    w_ada = din("w_ada", [D, 6 * D])
    b_ada = din("b_ada", [1, 6 * D])
    g_mix = din("g_mix", [1, D])
    w_in = din("w_in", [D, 6144])
    tbl = din("tbl", [5, 16, P, 5, P])
    hsw = din("hy_short_w", [3, 3072])
    hsb = din("hy_short_b", [1, 3072])
    fw1 = din("fw1", [33, 64])
    fb1 = din("fb1", [64, 1])
    fw2 = din("fw2", [64, 64])
    fb2 = din("fb2", [64, 1])
    fw3 = din("fw3", [64, 64])
    fb3 = din("fb3", [64, 1])
    fw4 = din("fw4", [64, 2048])
    ffq = din("ffq", [64, 1])
    hy_bias = din("hy_bias", [1, DHY])
    beta_na = din("beta_na", [1, DNA])
    beta_hy = din("beta_hy", [1, DHY])
    w_out = din("w_out", [D, D])
    g_ffn = din("g_ffn", [1, D])
    w_up = din("w_up", [D, 2 * DFF])
    fcw = din("ffn_conv_w", [3, DFF])
    fcb = din("ffn_conv_b", [1, DFF])
    w_down = din("w_down", [DFF, D])
    g_final = din("g_final", [1, D])
    zT = din("zT", [33, ZL])
    tft = din("tft", [1, ZL])
    tbt = din("tbt", [1, ZL])
    negdelta = din("negdelta", [1, DHY])
    ident_in = din("ident", [P, P])
    anti_in = din("anti", [P, P])
    out = nc.dram_tensor("out", [L, D], F32, kind="ExternalOutput").ap()

    Gd = dscr("Gd", [DHY, ZL], BF16)
    QTd = dscr("QTd", [DNA, L], BF16)
    KTd = dscr("KTd", [DNA, L], BF16)
    Vd = dscr("Vd", [L, DNA], BF16)
    X0d = dscr("X0d", [DHY, L], BF16)
    Ud = dscr("Ud", [L, DHY], BF16)
    HYd = dscr("HYd", [DHY, L], BF16)
    MNd = dscr("MNd", [DNA, L], BF16)
    X1d = dscr("X1d", [L, D], F32)
    MODd = dscr("MODd", [P, 96], F32)
    GTd = dscr("GTd", [DFF, L], BF16)

    K = Ctx(nc)
    cnt = [0]

    def sb(es, name, shape, dt):
        cnt[0] += 1
        return es.enter_context(nc.sbuf_tensor(f"{name}_{cnt[0]}", list(shape), dt))

    def ps(es, name, shape, dt):
        cnt[0] += 1
        return es.enter_context(nc.psum_tensor(f"{name}_{cnt[0]}", list(shape), dt))

    def colvec(src, n):
        return src.rearrange("o (k p) -> p (o k)", p=P)

    def bcast_rows(src_row, n):
        return bass.AP(tensor=src_row.tensor, offset=src_row.offset, ap=[[0, P], [1, n]])

    with ExitStack() as g:
        identb = sb(g, "identb", [P, P], BF16)
        antib = sb(g, "antib", [P, P], BF16)
        onesb = sb(g, "onesb", [P, P], BF16)
        onef = sb(g, "onef", [1, P], F32)
        modT = sb(g, "modT", [P, 96], F32)
        A1 = sb(g, "A1", [P, KC], F32)
        A2 = sb(g, "A2", [P, KC], F32)
        gt1b = sb(g, "gt1b", [P, D], F32)
        gt2b = sb(g, "gt2b", [P, D], F32)
        tmpc = sb(g, "tmpc", [P, KC], F32)

        K.dma("pool", "identb", identb[:], ident_in[:], writes=["identb"])
        K.dma("pool", "antib", antib[:], anti_in[:], writes=["antib"])
        K.op("pool", lambda e: e.memset(onesb[:], 1.0), writes=["onesb"])
        K.op("pool", lambda e: e.memset(onef[:], 1.0), writes=["onef"])

        with ExitStack() as es:
            K.barrier()
            cT = sb(es, "cT", [P, KC], F32)
            wa = [sb(es, f"wa{i}", [P, KC, 512], F32) for i in range(2)]
            ba = [sb(es, f"ba{i}", [1, 512], F32) for i in range(2)]
            row = [sb(es, f"row{i}", [1, 512], F32) for i in range(2)]
            psA = [ps(es, f"psA{i}", [1, 512], F32) for i in range(2)]
            psT = ps(es, "psT", [P, 96], F32)
            psB = ps(es, "psB", [P, 512], F32)
            K.dma("sp", "cT", cT[:], colvec(c, KC), writes=["cT"], allow_slow_non_contiguous=True)
            K.op("act", lambda e: e.activation(out=cT[:], in_=cT[:], func=AF.Silu), reads=["cT"], writes=["cT"])
            wav = w_ada.rearrange("(k p) n -> p k n", p=P)
            for jt in range(24):
                b = jt % 2
                K.dma("sp", f"wa{b}", wa[b][:], wav[:, :, jt * 512:(jt + 1) * 512], writes=[f"wa{b}"])
                K.dma("sp", f"ba{b}", ba[b][:], b_ada[:, jt * 512:(jt + 1) * 512], writes=[f"ba{b}"])
                for k in range(KC):
                    K.op("pe", lambda e, k=k, b=b: e.matmul(psA[b][:], cT[:, k:k + 1], wa[b][:, k, :],
                                                           start=(k == 0), stop=(k == KC - 1)),
                         reads=["cT", f"wa{b}"], writes=[f"psA{b}"], signal=(k == KC - 1))
                K.op("dve", lambda e, b=b: e.tensor_tensor(out=row[b][:], in0=psA[b][:], in1=ba[b][:], op=ALU.add),
                     reads=[f"psA{b}", f"ba{b}"], writes=[f"row{b}"])
                for m in range(4):
                    j = jt * 4 + m
                    K.op("pe", lambda e, m=m, j=j, b=b: e.matmul(psT[:, j:j + 1], row[b][0:1, m * P:(m + 1) * P],
                                                               onef[0:1, 0:1], start=True, stop=True),
                         reads=[f"row{b}", "onef"], writes=["psT"], signal=(m == 3))
                if 8 <= jt < 12 or 20 <= jt < 24:
                    dst = gt1b if jt < 12 else gt2b
                    o0 = (jt - 8) * 512 if jt < 12 else (jt - 20) * 512
                    K.op("pe", lambda e, b=b: e.matmul(psB[:], onef[0:1, :], row[b][0:1, :], start=True, stop=True),
                         reads=[f"row{b}", "onef"], writes=["psB"])
                    K.op("act", lambda e, dst=dst, o0=o0: e.copy(out=dst[:, o0:o0 + 512], in_=psB[:]),
                         reads=["psB"], writes=["gtb"])
            K.op("dve", lambda e: e.tensor_copy(out=modT[:], in_=psT[:]), reads=["psT"], writes=["modT"])
            if DEBUG:
                K.dma("sp", "modd", MODd[:], modT[:], reads=["modT"], writes=["MODd"])
            K.dma("sp", "tmpc", tmpc[:], colvec(g_mix, KC), writes=["tmpc"], allow_slow_non_contiguous=True)
            K.op("dve", lambda e: e.scalar_tensor_tensor(out=A1[:], in0=modT[:, 16:32], scalar=1.0, in1=tmpc[:],
                                                        op0=ALU.add, op1=ALU.mult),
                 reads=["modT", "tmpc"], writes=["A1"])
            K.dma("sp", "tmpc", tmpc[:], colvec(g_ffn, KC), writes=["tmpc"], allow_slow_non_contiguous=True)
            K.op("dve", lambda e: e.scalar_tensor_tensor(out=A2[:], in0=modT[:, 64:80], scalar=1.0, in1=tmpc[:],
                                                        op0=ALU.add, op1=ALU.mult),
                 reads=["modT", "tmpc"], writes=["A2"])

        with ExitStack() as es:
            K.barrier()
            Ha = sb(es, "Ha", [64, ZL], F32)
            w4s = sb(es, "w4s", [64, 2048], F32)
            bq = sb(es, "bq", [64, 4], F32)
            ndl = sb(es, "ndl", [P, 8], F32)
            dbs = sb(es, "dbs", [P, 8], F32)
            targ = [sb(es, f"targ{i}", [P, 512], F32) for i in range(2)]
            tk = [sb(es, f"tk{i}", [P, 512], F32) for i in range(2)]
            dcy = [sb(es, f"dcy{i}", [P, 512], F32) for i in range(2)]
            psF = [ps(es, f"psF{i}", [P, 512], F32) for i in range(4)]
            K.dma("sp", "w4s", w4s[:], fw4[:], writes=["w4s"])
            K.dma("sp", "bq0", bq[:, 0:1], fb1[:], writes=["bq"])
            K.dma("sp", "bq1", bq[:, 1:2], fb2[:], writes=["bq"])
            K.dma("sp", "bq2", bq[:, 2:3], fb3[:], writes=["bq"])
            K.dma("sp", "bq3", bq[:, 3:4], ffq[:], writes=["bq"])
            K.dma("sp", "ndl", ndl[:], colvec(negdelta, 8), writes=["ndl"], allow_slow_non_contiguous=True)
            K.dma("sp", "dbs", dbs[:], colvec(hy_bias, 8), writes=["dbs"], allow_slow_non_contiguous=True)
            with ExitStack() as e1:
                K.barrier()
                zsb = sb(e1, "zsb", [33, ZL], F32)
                Hb = sb(e1, "Hb", [64, ZL], F32)
                w1s = sb(e1, "w1s", [33, 64], F32)
                w2s = sb(e1, "w2s", [64, 64], F32)
                w3s = sb(e1, "w3s", [64, 64], F32)
                K.dma("sp", "zsb", zsb[:], zT[:], writes=["zsb"])
                K.dma("sp", "w1s", w1s[:], fw1[:], writes=["w1s"])
                K.dma("sp", "w2s", w2s[:], fw2[:], writes=["w2s"])
                K.dma("sp", "w3s", w3s[:], fw3[:], writes=["w3s"])
                layers = [(w1s, 33, zsb, "zsb", Ha, "Ha", 0), (w2s, 64, Ha, "Ha", Hb, "Hb", 1), (w3s, 64, Hb, "Hb", Ha, "Ha", 2)]
                it = 0
                for (ws, kk, src, sname, dst, dname, li) in layers:
                    for nt in range(16):
                        b = it % 2
                        pb = it % 4
                        it += 1
                        sl = slice(nt * 512, (nt + 1) * 512)
                        K.op("pe", lambda e, ws=ws, kk=kk, src=src, sl=sl, pb=pb: e.matmul(
                            psF[pb][0:64, :], ws[0:kk, :], src[0:kk, sl], start=True, stop=True),
                            reads=[sname, "w1s", "w2s", "w3s"], writes=[f"psF{pb}"])
                        K.op("dve", lambda e, b=b, pb=pb, li=li: e.tensor_scalar(
                            out=targ[b][0:64, :], in0=psF[pb][0:64, :], scalar1=bq[:, li:li + 1], scalar2=bq[:, 3:4],
                            op0=ALU.add, op1=ALU.mult), reads=[f"psF{pb}", "bq"], writes=[f"targ{b}"])
                        K.op("dve", lambda e, b=b: e.tensor_scalar(
                            out=tk[b][0:64, :], in0=targ[b][0:64, :], scalar1=float(1 / (2 * np.pi)), scalar2=12582912.0,
                            op0=ALU.mult, op1=ALU.add), reads=[f"targ{b}"], writes=[f"tk{b}"])
                        K.op("dve", lambda e, b=b: e.tensor_scalar(
                            out=tk[b][0:64, :], in0=tk[b][0:64, :], scalar1=-12582912.0, scalar2=float(-2 * np.pi),
                            op0=ALU.add, op1=ALU.mult), reads=[f"tk{b}"], writes=[f"tk{b}"])
                        K.op("dve", lambda e, b=b: e.tensor_tensor(
                            out=tk[b][0:64, :], in0=tk[b][0:64, :], in1=targ[b][0:64, :], op=ALU.add),
                            reads=[f"tk{b}", f"targ{b}"], writes=[f"tk{b}"])
                        K.op("dve", lambda e, b=b: e.tensor_scalar(
                            out=tk[b][0:64, :], in0=tk[b][0:64, :], scalar1=-PI, scalar2=PI,
                            op0=ALU.max, op1=ALU.min), reads=[f"tk{b}"], writes=[f"tk{b}"])
                        K.op("act", lambda e, b=b, dst=dst, sl=sl: e.activation(out=dst[:, sl], in_=tk[b][0:64, :], func=AF.Sin),
                             reads=[f"tk{b}"], writes=[dname])
            H3, h3n = Ha, "Ha"
            with ExitStack() as e1:
                K.barrier()
                tfbF = sb(e1, "tfbF", [P, ZL], F32)
                tfbB = sb(e1, "tfbB", [P, ZL], F32)
                gst = [sb(e1, f"gst{i}", [P, ZL], BF16) for i in range(2)]
                K.dma("sp", "tfbF", tfbF[:], bcast_rows(tft, ZL), writes=["tfbF"])
                K.dma("sp", "tfbB", tfbB[:], bcast_rows(tbt, ZL), writes=["tfbB"])
                it = 0
                for cc in range(8):
                    gb = cc % 2
                    for nt in range(16):
                        sl = slice(nt * 512, (nt + 1) * 512)
                        b = it % 2
                        it += 1
                        pf = 2 * b
                        pbk = 2 * b + 1
                        for side, pb in ((0, pf), (1, pbk)):
                            c0 = side * DHY + cc * P
                            K.op("pe", lambda e, pb=pb, c0=c0, sl=sl: e.matmul(psF[pb][:], w4s[:, c0:c0 + P], H3[:, sl],
                                                                              start=True, stop=True),
                                 reads=[h3n, "w4s"], writes=[f"psF{pb}"])
                        K.op("act", lambda e, b=b, sl=sl, cc=cc: e.activation(out=dcy[b][:], in_=tfbF[:, sl], func=AF.Exp,
                                                                             scale=ndl[:, cc:cc + 1]),
                             reads=["tfbF", "ndl"], writes=[f"dcy{b}"])
                        K.op("dve", lambda e, b=b, pf=pf: e.tensor_tensor(out=targ[b][:], in0=psF[pf][:], in1=dcy[b][:], op=ALU.mult),
                             reads=[f"psF{pf}", f"dcy{b}"], writes=[f"targ{b}"])
                        K.op("act", lambda e, b=b, sl=sl, cc=cc: e.activation(out=dcy[b][:], in_=tfbB[:, sl], func=AF.Exp,
                                                                             scale=ndl[:, cc:cc + 1]),
                             reads=["tfbB", "ndl"], writes=[f"dcy{b}"])
                        K.op("dve", lambda e, b=b, pbk=pbk: e.tensor_tensor(out=tk[b][:], in0=psF[pbk][:], in1=dcy[b][:], op=ALU.mult),
                             reads=[f"psF{pbk}", f"dcy{b}"], writes=[f"tk{b}"])
                        if nt == 7:
                            K.op("dve", lambda e, b=b, cc=cc: e.tensor_scalar(out=targ[b][:, 511:512], in0=targ[b][:, 511:512],
                                                                             scalar1=dbs[:, cc:cc + 1], scalar2=0.0,
                                                                             op0=ALU.add, op1=ALU.add),
                                 reads=[f"targ{b}", "dbs"], writes=[f"targ{b}"])
                        K.op("dve", lambda e, b=b, gb=gb, sl=sl: e.tensor_tensor(out=gst[gb][:, sl], in0=targ[b][:], in1=tk[b][:], op=ALU.add),
                             reads=[f"targ{b}", f"tk{b}"], writes=[f"gst{gb}"])
                    K.dma("sp", f"gst{gb}", Gd[cc * P:(cc + 1) * P, :], gst[gb][:], reads=[f"gst{gb}"], writes=["Gd"])

        def norm_block(xt, xtn, xn, ss, pst, hT, col0, A, sh_lo, hname):
            K.op("act", lambda e: e.activation(out=xn[:], in_=xt[:], func=AF.Square, accum_out=ss[:, 0:1]),
                 reads=[xtn], writes=["xn", "ss"])
            K.op("dve", lambda e: e.tensor_scalar(out=ss[:, 1:2], in0=ss[:, 0:1], scalar1=1.0 / D, scalar2=EPS,
                                                 op0=ALU.mult, op1=ALU.add), reads=["ss"], writes=["ss"])
            K.op("act", lambda e: e.activation(out=ss[:, 2:3], in_=ss[:, 1:2], func=AF.Sqrt), reads=["ss"], writes=["ss"])
            K.op("dve", lambda e: e.reciprocal(out=ss[:, 3:4], in_=ss[:, 2:3]), reads=["ss"], writes=["ss"])
            K.op("act", lambda e: e.activation(out=xn[:], in_=xt[:], func=AF.Copy, scale=ss[:, 3:4]),
                 reads=[xtn, "ss"], writes=["xn"])
            for hh in range(2):
                for k8 in range(8):
                    k = hh * 8 + k8
                    K.op("pe", lambda e, k=k, k8=k8, hh=hh: e.transpose(pst[hh][:, k8, :], xn[:, k * P:(k + 1) * P], identb[:]),
                         reads=["xn", "identb"], writes=[f"pst{hh}"], signal=(k8 == 7))
                for k8 in range(8):
                    k = hh * 8 + k8
                    if k % 2 == 0:
                        K.op("dve", lambda e, k=k, k8=k8, hh=hh: e.tensor_scalar(
                            out=hT[:, k, col0:col0 + P], in0=pst[hh][:, k8, :], scalar1=A[:, k:k + 1],
                            scalar2=modT[:, sh_lo + k:sh_lo + k + 1], op0=ALU.mult, op1=ALU.add),
                            reads=[f"pst{hh}", "A1", "A2", "modT"], writes=[hname])
                    else:
                        K.op("act", lambda e, k=k, k8=k8, hh=hh: e.activation(
                            out=hT[:, k, col0:col0 + P], in_=pst[hh][:, k8, :], func=AF.Identity,
                            scale=A[:, k:k + 1], bias=modT[:, sh_lo + k:sh_lo + k + 1]),
                            reads=[f"pst{hh}", "A1", "A2", "modT"], writes=[hname])

        def conv3(eng, dst, rowb, n, w3, bcol, tmp, rname, tname, dname):
            K.op(eng, lambda e: e.tensor_scalar(out=tmp[:, 0:n], in0=rowb[:, 1:n + 1], scalar1=w3[1], scalar2=bcol,
                                                op0=ALU.mult, op1=ALU.add), reads=[rname, "cw"], writes=[tname])
            K.op(eng, lambda e: e.scalar_tensor_tensor(out=tmp[:, 0:n], in0=rowb[:, 0:n], scalar=w3[0], in1=tmp[:, 0:n],
                                                       op0=ALU.mult, op1=ALU.add), reads=[rname, tname, "cw"], writes=[tname])
            K.op(eng, lambda e: e.scalar_tensor_tensor(out=dst, in0=rowb[:, 2:n + 2], scalar=w3[2], in1=tmp[:, 0:n],
                                                       op0=ALU.mult, op1=ALU.add), reads=[rname, tname, "cw"], writes=[dname])

        def row_project(hT, hname, wt, wname, kidx, rowb, rname, q, psr, evac_i):
            t0 = 1024 * q - 1
            lo = max(t0, 0)
            hi = min(t0 + 1026, L)
            if q == 0:
                K.op("pool", lambda e: e.memset(rowb[:, 0:1], 0.0), writes=[rname])
            if q == 3:
                K.op("pool", lambda e: e.memset(rowb[:, 1025:1026], 0.0), writes=[rname])
            pos = lo
            while pos < hi:
                n = min(342, hi - pos)
                pb = evac_i[0] % len(psr)
                evac_i[0] += 1
                for k in range(KC):
                    K.op("pe", lambda e, k=k, pb=pb, pos=pos, n=n: e.matmul(
                        psr[pb][:, 0:n], wt[:, k, kidx * P:(kidx + 1) * P], hT[:, k, pos:pos + n],
                        start=(k == 0), stop=(k == KC - 1)),
                        reads=[hname, wname], writes=[f"psr{pb}"], signal=(k == KC - 1))
                en = "act" if evac_i[0] % 2 == 0 else "dve"
                if en == "act":
                    K.op("act", lambda e, pb=pb, pos=pos, n=n: e.copy(out=rowb[:, pos - t0:pos - t0 + n], in_=psr[pb][:, 0:n]),
                         reads=[f"psr{pb}"], writes=[rname])
                else:
                    K.op("dve", lambda e, pb=pb, pos=pos, n=n: e.tensor_copy(out=rowb[:, pos - t0:pos - t0 + n], in_=psr[pb][:, 0:n]),
                         reads=[f"psr{pb}"], writes=[rname])
                pos += n

        with ExitStack() as es:
            K.barrier()
            hT = sb(es, "hT", [P, KC, L], BF16)
            cw = sb(es, "cw", [P, 3, 24], F32)
            cb = sb(es, "cb", [P, 24], F32)
            for j3 in range(3):
                K.dma("sp", f"cw_{j3}", cw[:, j3, :], colvec(hsw[j3:j3 + 1, :], 24), writes=["cw"], allow_slow_non_contiguous=True)
            K.dma("sp", "cb", cb[:], colvec(hsb, 24), writes=["cw"], allow_slow_non_contiguous=True)
            with ExitStack() as e2:
                K.barrier()
                xt = [sb(e2, f"xt{i}", [P, D], F32) for i in range(2)]
                xn = sb(e2, "xn", [P, D], BF16)
                ss = sb(e2, "ss", [P, 4], F32)
                pst = [ps(e2, f"pst{i}", [P, 8, P], BF16) for i in range(2)]
                for blk in range(NB):
                    b = blk % 2
                    K.dma("sp", f"xt{b}", xt[b][:], x[blk * P:(blk + 1) * P, :], writes=[f"xt{b}"])
                    norm_block(xt[b], f"xt{b}", xn, ss, pst, hT, blk * P, A1, 0, "hT")
            with ExitStack() as e2:
                K.barrier()
                GW = 256
                wt = [sb(e2, f"wt{i}", [P, KC, GW], BF16) for i in range(3)]
                stg = [sb(e2, f"stg{i}", [P, 1024], BF16) for i in range(2)]
                rowA = sb(e2, "rowA", [P, 1026], F32)
                rowB = sb(e2, "rowB", [P, 1026], F32)
                ctmp = sb(e2, "ctmp", [P, 1024], F32)
                ca = sb(e2, "ca", [P, 1024], F32)
                ufm = sb(e2, "ufm", [P, 1024], BF16)
                utm = sb(e2, "utm", [P, 8, P], BF16)
                psr = [ps(e2, f"psr{i}", [P, 512], F32) for i in range(4)]
                psu = ps(e2, "psu", [P, 8, P], BF16)
                wiv = w_in.rearrange("(k p) n -> p k n", p=P)
                ei = [0]
                wi = [0]

                def load_w(g2):
                    b = wi[0] % 3
                    wi[0] += 1
                    K.dma("pool", f"wt{b}", wt[b][:], wiv[:, :, g2 * GW:(g2 + 1) * GW], writes=[f"wt{b}"])
                    return b

                for g2 in range(8):
                    b = load_w(g2)
                    dst = QTd if g2 < 4 else KTd
                    for kidx in range(2):
                        r0 = ((g2 % 4) * 2 + kidx) * P
                        for half in range(4):
                            sgi = ei[0] % 2
                            for nt in range(2):
                                pb = ei[0] % 4
                                ei[0] += 1
                                pos = half * 1024 + nt * 512
                                for k in range(KC):
                                    K.op("pe", lambda e, k=k, pb=pb, pos=pos, b=b, kidx=kidx: e.matmul(
                                        psr[pb][:], wt[b][:, k, kidx * P:(kidx + 1) * P], hT[:, k, pos:pos + 512],
                                        start=(k == 0), stop=(k == KC - 1)),
                                        reads=["hT", f"wt{b}"], writes=[f"psr{pb}"], signal=(k == KC - 1))
                                sc = 0.125 if g2 < 4 else 1.0
                                K.op("act", lambda e, pb=pb, nt=nt, sgi=sgi, sc=sc: e.activation(
                                    out=stg[sgi][:, nt * 512:(nt + 1) * 512], in_=psr[pb][:], func=AF.Copy, scale=sc),
                                    reads=[f"psr{pb}"], writes=[f"stg{sgi}"])
                            K.dma("sp", f"stg{sgi}", dst[r0:r0 + P, half * 1024:(half + 1) * 1024], stg[sgi][:],
                                  reads=[f"stg{sgi}"], writes=["QKd"])
                Vv = Vd.rearrange("(j p) c -> p j c", p=P)
                for g2 in range(8, 12):
                    b = load_w(g2)
                    for blk in range(NB):
                        pb = ei[0] % 4
                        sgi = ei[0] % 2
                        ei[0] += 1
                        for k in range(KC):
                            K.op("pe", lambda e, k=k, pb=pb, blk=blk, b=b: e.matmul(
                                psr[pb][:, 0:GW], hT[:, k, blk * P:(blk + 1) * P], wt[b][:, k, :],
                                start=(k == 0), stop=(k == KC - 1)),
                                reads=["hT", f"wt{b}"], writes=[f"psr{pb}"], signal=(k == KC - 1))
                        K.op("dve", lambda e, pb=pb, sgi=sgi: e.tensor_copy(out=stg[sgi][:, 0:GW], in_=psr[pb][:, 0:GW]),
                             reads=[f"psr{pb}"], writes=[f"stg{sgi}"])
                        K.dma("sp", f"stg{sgi}", Vv[:, blk, (g2 - 8) * GW:(g2 - 7) * GW], stg[sgi][:, 0:GW],
                              reads=[f"stg{sgi}"], writes=["Vd"])
                for g2 in range(12, 16):
                    b = load_w(g2)
                    for kidx in range(2):
                        ch = (g2 - 12) * 2 + kidx
                        for q in range(4):
                            row_project(hT, "hT", wt[b], f"wt{b}", kidx, rowA, "rowA", q, psr, ei)
                            sgi = ei[0] % 2
                            conv3("dve", stg[sgi][:, 0:1024], rowA, 1024,
                                  [cw[:, 0, ch:ch + 1], cw[:, 1, ch:ch + 1], cw[:, 2, ch:ch + 1]], cb[:, ch:ch + 1],
                                  ctmp, "rowA", "ctmp", f"stg{sgi}")
                            K.dma("sp", f"stg{sgi}", X0d[ch * P:(ch + 1) * P, q * 1024:(q + 1) * 1024], stg[sgi][:],
                                  reads=[f"stg{sgi}"], writes=["X0d"])
                Uv = Ud.rearrange("(j p) c -> p j c", p=P)
                for gg in range(4):
                    b1 = load_w(16 + gg)
                    b2 = load_w(20 + gg)
                    for kidx in range(2):
                        ch1 = 8 + gg * 2 + kidx
                        ch2 = 16 + gg * 2 + kidx
                        cc = gg * 2 + kidx
                        for q in range(4):
                            row_project(hT, "hT", wt[b1], f"wt{b1}", kidx, rowA, "rowA", q, psr, ei)
                            row_project(hT, "hT", wt[b2], f"wt{b2}", kidx, rowB, "rowB", q, psr, ei)
                            conv3("dve", ca[:, 0:1024], rowA, 1024,
                                  [cw[:, 0, ch1:ch1 + 1], cw[:, 1, ch1:ch1 + 1], cw[:, 2, ch1:ch1 + 1]], cb[:, ch1:ch1 + 1],
                                  ctmp, "rowA", "ctmp", "ca")
                            conv3("dve", ctmp[:, 0:1024], rowB, 1024,
                                  [cw[:, 0, ch2:ch2 + 1], cw[:, 1, ch2:ch2 + 1], cw[:, 2, ch2:ch2 + 1]], cb[:, ch2:ch2 + 1],
                                  rowA, "rowB", "rowA", "ctmp")
                            K.op("dve", lambda e: e.tensor_tensor(out=ufm[:], in0=ca[:], in1=ctmp[:], op=ALU.mult),
                                 reads=["ca", "ctmp"], writes=["ufm"])
                            for j in range(8):
                                K.op("pe", lambda e, j=j: e.transpose(psu[:, j, :], ufm[:, j * P:(j + 1) * P], identb[:]),
                                     reads=["ufm", "identb"], writes=["psu"], signal=(j == 7))
                            K.op("act", lambda e: e.copy(out=utm[:], in_=psu[:]), reads=["psu"], writes=["utm"])
                            K.dma("sp", "utm", Uv[:, q * 8:(q + 1) * 8, cc * P:(cc + 1) * P], utm[:],
                                  reads=["utm"], writes=["Ud"])

        with ExitStack() as es:
            K.barrier()
            upad = [sb(es, f"upad{i}", [P, 96, P], BF16) for i in range(2)]
            band = [sb(es, f"band{i}", [P, 8064], BF16) for i in range(3)]
            ytm = sb(es, "ytm", [P, NB, P], BF16)
            x0c = sb(es, "x0c", [P, L], BF16)
            hy = sb(es, "hy", [P, L], BF16)
            psY = [ps(es, f"psY{i}", [P, 16, NB], F32) for i in range(2)]
            psZ = [ps(es, f"psZ{i}", [P, 4, P], F32) for i in range(2)]
            Uv = Ud.rearrange("(j p) c -> p j c", p=P)
            for i in range(2):
                K.op("pool", lambda e, i=i: e.memset(upad[i][:], 0.0), writes=[f"upad{i}"])
            bi = 0
            for cg in range(8):
                ub = cg % 2
                K.dma("sp", f"upad{ub}", upad[ub][:, 32:64, :], Uv[:, :, cg * P:(cg + 1) * P],
                      reads=["Ud"], writes=[f"upad{ub}"])
                K.dma("sp", "x0c", x0c[:], X0d[cg * P:(cg + 1) * P, :], reads=["X0d"], writes=["x0c"])
                for cl in range(P):
                    ch = cg * P + cl
                    bb = bi % 3
                    bi += 1
                    src = bass.AP(tensor=Gd.tensor, offset=ch * ZL, ap=[[1, P], [1, 8064]])
                    K.dma("sp", f"band{bb}", band[bb][:], src, reads=["Gd"], writes=[f"band{bb}"])
                    yb = (cl // 16) % 2
                    c16 = cl % 16
                    for e_ in range(63):
                        K.op("pe", lambda e, e_=e_, bb=bb, ub=ub, cl=cl, yb=yb, c16=c16: e.matmul(
                            psY[yb][:, c16, :], band[bb][:, e_ * P:(e_ + 1) * P], upad[ub][:, e_ + 1:e_ + 33, cl],
                            start=(e_ == 0), stop=(e_ == 62)),
                            reads=[f"band{bb}", f"upad{ub}"], writes=[f"psY{yb}"], signal=(e_ == 62))
                    if c16 == 15:
                        c0 = cl - 15
                        K.op("act", lambda e, yb=yb, c0=c0: e.copy(out=ytm[:, :, c0:c0 + 16],
                                                                 in_=psY[yb][:].rearrange("p c i -> p i c")),
                             reads=[f"psY{yb}"], writes=["ytm"])
                for i4 in range(8):
                    zb = i4 % 2
                    for ii in range(4):
                        i = i4 * 4 + ii
                        K.op("pe", lambda e, i=i, ii=ii, zb=zb: e.matmul(psZ[zb][:, ii, :], ytm[:, i, :], antib[:],
                                                                       start=True, stop=True),
                             reads=["ytm", "antib"], writes=[f"psZ{zb}"], signal=(ii == 3))
                    K.op("dve", lambda e, i4=i4, zb=zb: e.tensor_tensor(
                        out=hy[:, i4 * 512:(i4 + 1) * 512], in0=psZ[zb][:].rearrange("p a b -> p (a b)"),
                        in1=x0c[:, i4 * 512:(i4 + 1) * 512], op=ALU.mult),
                        reads=[f"psZ{zb}", "x0c"], writes=["hy"])
                K.dma("sp", "hy", HYd[cg * P:(cg + 1) * P, :], hy[:], reads=["hy"], writes=["HYd"])

        with ExitStack() as es:
            K.barrier()
            na = sb(es, "na", [P, NB, DNA], BF16)
            qt = [sb(es, f"qt{i}", [64, L], BF16) for i in range(2)]
            kt = [sb(es, f"kt{i}", [64, L], BF16) for i in range(2)]
            vh = [sb(es, f"vh{i}", [P, NB, 65], BF16) for i in range(2)]
            tb_ = [sb(es, f"tb{i}", [P, 5, 5, P], BF16) for i in range(2)]
            pt = [sb(es, f"pt{i}", [P, 5, P], BF16) for i in range(2)]
            rc = [sb(es, f"rc{i}", [P, 1], F32) for i in range(2)]
            psS = [ps(es, f"psS{i}", [P, 4, P], F32) for i in range(2)]
            psS2 = [ps(es, f"psS2{i}", [P, P], F32) for i in range(2)]
            psO = [ps(es, f"psO{i}", [P, 65], F32) for i in range(2)]
            Vv = Vd.rearrange("(j p) c -> p j c", p=P)
            for i in range(2):
                K.op("pool", lambda e, i=i: e.memset(vh[i][:, :, 64:65], 1.0), writes=[f"vh{i}"])
            it = 0
            for hd in range(16):
                hb = hd % 2
                K.dma("sp", f"qt{hb}", qt[hb][:], QTd[hd * 64:(hd + 1) * 64, :], reads=["QKd"], writes=[f"qt{hb}"])
                K.dma("sp", f"kt{hb}", kt[hb][:], KTd[hd * 64:(hd + 1) * 64, :], reads=["QKd"], writes=[f"kt{hb}"])
                K.dma("sp", f"vh{hb}", vh[hb][:, :, 0:64], Vv[:, :, hd * 64:(hd + 1) * 64], reads=["Vd"], writes=[f"vh{hb}"])
                for cl5 in range(5):
                    K.dma("pool", f"tb{hb}_{cl5}", tb_[hb][:, cl5, :, :], tbl[cl5, hd], writes=[f"tb{hb}"])
                for R in range(NB):
                    kb0 = min(max(R - 2, 0), 27)
                    cls = {0: 0, 1: 1, 30: 3, 31: 4}.get(R, 2)
                    b = it % 2
                    it += 1
                    for kbi in range(5):
                        dstp = psS[b][:, kbi, :] if kbi < 4 else psS2[b][:]
                        wn = f"psS{b}" if kbi < 4 else f"psS2{b}"
                        K.op("pe", lambda e, dstp=dstp, kbi=kbi, hb=hb, kb0=kb0, R=R: e.matmul(
                            dstp, kt[hb][:, (kb0 + kbi) * P:(kb0 + kbi + 1) * P], qt[hb][:, R * P:(R + 1) * P],
                            start=True, stop=False), reads=[f"kt{hb}", f"qt{hb}"], writes=[wn], signal=False)
                        K.op("pe", lambda e, dstp=dstp, kbi=kbi, hb=hb, cls=cls: e.matmul(
                            dstp, identb[:], tb_[hb][:, cls, kbi, :], start=False, stop=True),
                            reads=["identb", f"tb{hb}"], writes=[wn], signal=(kbi >= 3))
                    K.op("act", lambda e, b=b: e.activation(out=pt[b][:, 0:4, :], in_=psS[b][:], func=AF.Exp),
                         reads=[f"psS{b}"], writes=[f"pt{b}"])
                    K.op("act", lambda e, b=b: e.activation(out=pt[b][:, 4, :], in_=psS2[b][:], func=AF.Exp),
                         reads=[f"psS2{b}"], writes=[f"pt{b}"])
                    for kbi in range(5):
                        K.op("pe", lambda e, kbi=kbi, b=b, hb=hb, kb0=kb0: e.matmul(
                            psO[b][:], pt[b][:, kbi, :], vh[hb][:, kb0 + kbi, :], start=(kbi == 0), stop=(kbi == 4)),
                            reads=[f"pt{b}", f"vh{hb}"], writes=[f"psO{b}"], signal=(kbi == 4))
                    K.op("dve", lambda e, b=b: e.reciprocal(out=rc[b][:], in_=psO[b][:, 64:65]),
                         reads=[f"psO{b}"], writes=[f"rc{b}"])
                    K.op("dve", lambda e, b=b, R=R, hd=hd: e.tensor_scalar(
                        out=na[:, R, hd * 64:(hd + 1) * 64], in0=psO[b][:, 0:64], scalar1=rc[b][:, 0:1], scalar2=1.0,
                        op0=ALU.mult, op1=ALU.mult), reads=[f"psO{b}", f"rc{b}"], writes=["na"])
            bnb = sb(es, "bnb", [P, DNA], F32)
            jk = sb(es, "jk", [P, DNA], BF16)
            nn = sb(es, "nn", [P, DNA], BF16)
            ss = sb(es, "ssn", [P, 4], F32)
            mst = [sb(es, f"mst{i}", [P, 8, P], BF16) for i in range(2)]
            psM = ps(es, "psM", [P, 8, P], BF16)
            K.dma("sp", "bnb", bnb[:], bcast_rows(beta_na, DNA), writes=["bnb"])
            for R in range(NB):
                K.op("act", lambda e, R=R: e.activation(out=jk[:], in_=na[:, R, :], func=AF.Square, accum_out=ss[:, 0:1]),
                     reads=["na"], writes=["jk", "ssn"])
                K.op("dve", lambda e: e.tensor_scalar(out=ss[:, 1:2], in0=ss[:, 0:1], scalar1=1.0 / DNA, scalar2=EPS,
                                                     op0=ALU.mult, op1=ALU.add), reads=["ssn"], writes=["ssn"])
                K.op("act", lambda e: e.activation(out=ss[:, 2:3], in_=ss[:, 1:2], func=AF.Sqrt), reads=["ssn"], writes=["ssn"])
                K.op("dve", lambda e: e.reciprocal(out=ss[:, 3:4], in_=ss[:, 2:3]), reads=["ssn"], writes=["ssn"])
                K.op("dve", lambda e, R=R: e.scalar_tensor_tensor(out=nn[:], in0=na[:, R, :], scalar=ss[:, 3:4], in1=bnb[:],
                                                                 op0=ALU.mult, op1=ALU.mult),
                     reads=["na", "ssn", "bnb"], writes=["nn"])
                for j in range(8):
                    K.op("pe", lambda e, j=j: e.transpose(psM[:, j, :], nn[:, j * P:(j + 1) * P], identb[:]),
                         reads=["nn", "identb"], writes=["psM"], signal=(j == 7))
                mb = R % 2
                K.op("act", lambda e, mb=mb: e.copy(out=mst[mb][:], in_=psM[:]), reads=["psM"], writes=[f"mst{mb}"])
                K.dma("sp", f"mst{mb}", MNd.rearrange("(j p) t -> p j t", p=P)[:, :, R * P:(R + 1) * P], mst[mb][:],
                      reads=[f"mst{mb}"], writes=["MNd"])

        with ExitStack() as es:
            K.barrier()
            wo = sb(es, "wo", [P, KC, D], BF16)
            bh = sb(es, "bh", [P, 8], F32)
            mt = [sb(es, f"mt{i}", [P, KC, 512], BF16) for i in range(2)]
            sq = sb(es, "sq", [P, 8, 512], BF16)
            rs = sb(es, "rs", [P, 512], F32)
            xr = [sb(es, f"xr{i}", [P, D], F32) for i in range(2)]
            x1 = [sb(es, f"x1{i}", [P, D], F32) for i in range(2)]
            psq = ps(es, "psq", [P, 512], F32)
            pso = [ps(es, f"pso{i}", [P, 512], F32) for i in range(4)]
            wov = w_out.rearrange("(k p) n -> p k n", p=P)
            for k4 in range(4):
                K.dma("pool", f"wo{k4}", wo[:, k4 * 4:(k4 + 1) * 4, :], wov[:, k4 * 4:(k4 + 1) * 4, :], writes=["wo"])
            K.dma("sp", "bh", bh[:], colvec(beta_hy, 8), writes=["bh"], allow_slow_non_contiguous=True)
            MNv = MNd.rearrange("(j p) t -> p j t", p=P)
            HYv = HYd.rearrange("(j p) t -> p j t", p=P)
            oi = 0
            for tt in range(8):
                b = tt % 2
                tsl = slice(tt * 512, (tt + 1) * 512)
                K.dma("sp", f"mtA{b}", mt[b][:, 0:8, :], MNv[:, :, tsl], reads=["MNd"], writes=[f"mt{b}"])
                K.dma("sp", f"mtB{b}", mt[b][:, 8:16, :], HYv[:, :, tsl], reads=["HYd"], writes=[f"mt{b}"])
                K.op("dve", lambda e, b=b: e.tensor_tensor(out=sq[:], in0=mt[b][:, 8:16, :], in1=mt[b][:, 8:16, :], op=ALU.mult),
                     reads=[f"mt{b}"], writes=["sq"])
                for j in range(8):
                    K.op("pe", lambda e, j=j: e.matmul(psq[:], onesb[:], sq[:, j, :], start=(j == 0), stop=(j == 7)),
                         reads=["sq", "onesb"], writes=["psq"], signal=(j == 7))
                K.op("dve", lambda e: e.tensor_scalar(out=rs[:], in0=psq[:], scalar1=1.0 / DHY, scalar2=EPS,
                                                     op0=ALU.mult, op1=ALU.add), reads=["psq"], writes=["rs"])
                K.op("act", lambda e: e.activation(out=rs[:], in_=rs[:], func=AF.Sqrt), reads=["rs"], writes=["rs"])
                K.op("dve", lambda e: e.reciprocal(out=rs[:], in_=rs[:]), reads=["rs"], writes=["rs"])
                for j in range(8):
                    K.op("dve", lambda e, j=j, b=b: e.scalar_tensor_tensor(
                        out=mt[b][:, 8 + j, :], in0=mt[b][:, 8 + j, :], scalar=bh[:, j:j + 1], in1=rs[:],
                        op0=ALU.mult, op1=ALU.mult), reads=[f"mt{b}", "bh", "rs"], writes=[f"mt{b}"])
                for bl in range(4):
                    blk = tt * 4 + bl
                    xb = blk % 2
                    K.dma("sp", f"xr{xb}", xr[xb][:], x[blk * P:(blk + 1) * P, :], writes=[f"xr{xb}"])
                    for ct in range(4):
                        pb = oi % 4
                        oi += 1
                        for k in range(KC):
                            K.op("pe", lambda e, k=k, pb=pb, b=b, bl=bl, ct=ct: e.matmul(
                                pso[pb][:], mt[b][:, k, bl * P:(bl + 1) * P], wo[:, k, ct * 512:(ct + 1) * 512],
                                start=(k == 0), stop=(k == KC - 1)),
                                reads=[f"mt{b}", "wo"], writes=[f"pso{pb}"], signal=(k == KC - 1))
                        K.op("dve", lambda e, pb=pb, xb=xb, ct=ct: e.tensor_tensor(
                            out=x1[xb][:, ct * 512:(ct + 1) * 512], in0=pso[pb][:], in1=gt1b[:, ct * 512:(ct + 1) * 512],
                            op=ALU.mult), reads=[f"pso{pb}", "gtb"], writes=[f"x1{xb}"])
                    K.op("pool", lambda e, xb=xb: e.tensor_tensor(out=x1[xb][:], in0=x1[xb][:], in1=xr[xb][:], op=ALU.add),
                         reads=[f"x1{xb}", f"xr{xb}"], writes=[f"x1{xb}"])
                    K.dma("sp", f"x1{xb}", X1d[blk * P:(blk + 1) * P, :], x1[xb][:], reads=[f"x1{xb}"], writes=["X1d"])

        with ExitStack() as es:
            K.barrier()
            hT = sb(es, "hT2", [P, KC, L], BF16)
            cw2 = sb(es, "cw2", [P, 3, NFC], F32)
            cb2 = sb(es, "cb2", [P, NFC], F32)
            for j3 in range(3):
                K.dma("sp", f"cw2_{j3}", cw2[:, j3, :], colvec(fcw[j3:j3 + 1, :], NFC), writes=["cw"], allow_slow_non_contiguous=True)
            K.dma("sp", "cb2", cb2[:], colvec(fcb, NFC), writes=["cw"], allow_slow_non_contiguous=True)
            with ExitStack() as e2:
                K.barrier()
                xt = [sb(e2, f"xu{i}", [P, D], F32) for i in range(2)]
                xn = sb(e2, "xn2", [P, D], BF16)
                ss = sb(e2, "ss2", [P, 4], F32)
                pst = [ps(e2, f"pst2{i}", [P, 8, P], BF16) for i in range(2)]
                for blk in range(NB):
                    b = blk % 2
                    K.dma("sp", f"xt{b}", xt[b][:], X1d[blk * P:(blk + 1) * P, :], reads=["X1d"], writes=[f"xt{b}"])
                    norm_block(xt[b], f"xt{b}", xn, ss, pst, hT, blk * P, A2, 48, "hT")
            with ExitStack() as e2:
                K.barrier()
                wa_ = [sb(e2, f"wua{i}", [P, KC, P], BF16) for i in range(2)]
                wb_ = [sb(e2, f"wub{i}", [P, KC, P], BF16) for i in range(2)]
                rowA = sb(e2, "rowA2", [P, 1026], F32)
                rowB = sb(e2, "rowB2", [P, 1026], F32)
                ctmp = sb(e2, "ctmp2", [P, 1024], F32)
                ca = sb(e2, "ca2", [P, 1024], F32)
                gs = [sb(e2, f"gs{i}", [P, 1024], BF16) for i in range(2)]
                psr = [ps(e2, f"psr2{i}", [P, 512], F32) for i in range(4)]
                wuv = w_up.rearrange("(k p) n -> p k n", p=P)
                ei = [0]
                for fc in range(NFC):
                    b = fc % 2
                    K.dma("pool", f"wua{b}", wa_[b][:], wuv[:, :, fc * P:(fc + 1) * P], writes=[f"wua{b}"])
                    K.dma("pool", f"wub{b}", wb_[b][:], wuv[:, :, DFF + fc * P:DFF + (fc + 1) * P], writes=[f"wub{b}"])
                    for q in range(4):
                        row_project(hT, "hT", wa_[b], f"wua{b}", 0, rowA, "rowA", q, psr, ei)
                        conv3("dve", ca[:, 0:1024], rowA, 1024,
                              [cw2[:, 0, fc:fc + 1], cw2[:, 1, fc:fc + 1], cw2[:, 2, fc:fc + 1]], cb2[:, fc:fc + 1],
                              ctmp, "rowA", "ctmp", "ca")
                        K.op("act", lambda e: e.activation(out=ca[:], in_=ca[:], func=AF.Gelu), reads=["ca"], writes=["ca"])
                        for nt in range(2):
                            pb = ei[0] % 4
                            ei[0] += 1
                            pos = q * 1024 + nt * 512
                            for k in range(KC):
                                K.op("pe", lambda e, k=k, pb=pb, pos=pos, b=b: e.matmul(
                                    psr[pb][:], wb_[b][:, k, :], hT[:, k, pos:pos + 512],
                                    start=(k == 0), stop=(k == KC - 1)),
                                    reads=["hT", f"wub{b}"], writes=[f"psr{pb}"], signal=(k == KC - 1))
                            gi = (fc * 4 + q) % 2
                            K.op("dve", lambda e, pb=pb, nt=nt, gi=gi: e.tensor_tensor(
                                out=gs[gi][:, nt * 512:(nt + 1) * 512], in0=psr[pb][:], in1=ca[:, nt * 512:(nt + 1) * 512],
                                op=ALU.mult), reads=[f"psr{pb}", "ca"], writes=[f"gs{gi}"])
                        K.dma("sp", f"gs{gi}", GTd[fc * P:(fc + 1) * P, q * 1024:(q + 1) * 1024], gs[gi][:],
                              reads=[f"gs{gi}"], writes=["GTd"])

        with ExitStack() as es:
            K.barrier()
            CW = 256
            gt = sb(es, "gtD", [P, NFC, 512], BF16)
            wd = [sb(es, f"wd{i}", [P, NFC, CW], BF16) for i in range(2)]
            x1 = sb(es, "x1r", [P, 4, D], F32)
            gfb = sb(es, "gfb", [P, D], F32)
            jk = sb(es, "jkd", [P, D], BF16)
            jk_f = sb(es, "jkf", [P, CW], F32)
            ss = sb(es, "ssd", [P, 4], F32)
            ot = [sb(es, f"ot{i}", [P, D], F32) for i in range(2)]
            psd = [ps(es, f"psd{i}", [P, 512], F32) for i in range(4)]
            K.dma("sp", "gfb", gfb[:], bcast_rows(g_final, D), writes=["gfb"])
            GTv = GTd.rearrange("(j p) t -> p j t", p=P)
            wdv = w_down.rearrange("(j p) n -> p j n", p=P)
            X1v = X1d.rearrange("(j p) d -> p j d", p=P)
            wi = 0
            oi = 0
            for tt in range(8):
                tsl = slice(tt * 512, (tt + 1) * 512)
                K.dma("sp", "gt0a", gt[:, 0:22, :], GTv[:, 0:22, tsl], reads=["GTd"], writes=["gt0"])
                K.dma("sp", "gt0b", gt[:, 22:44, :], GTv[:, 22:44, tsl], reads=["GTd"], writes=["gt0"])
                K.dma("sp", "x1r", x1[:], X1v[:, tt * 4:(tt + 1) * 4, :], reads=["X1d"], writes=["x1r"])
                for ct in range(D // CW):
                    wb = wi % 2
                    wi += 1
                    csl = slice(ct * CW, (ct + 1) * CW)
                    K.dma("pool", f"wd{wb}a", wd[wb][:, 0:22, :], wdv[:, 0:22, csl], writes=[f"wd{wb}"])
                    K.dma("pool", f"wd{wb}b", wd[wb][:, 22:44, :], wdv[:, 22:44, csl], writes=[f"wd{wb}"])
                    for bl in range(4):
                        pb = oi % 4
                        oi += 1
                        for j in range(NFC):
                            K.op("pe", lambda e, j=j, pb=pb, bl=bl, wb=wb: e.matmul(
                                psd[pb][:, 0:CW], gt[:, j, bl * P:(bl + 1) * P], wd[wb][:, j, :],
                                start=(j == 0), stop=(j == NFC - 1)),
                                reads=["gt0", f"wd{wb}"], writes=[f"psd{pb}"], signal=(j == NFC - 1))
                        K.op("dve", lambda e, pb=pb, csl=csl: e.tensor_tensor(
                            out=jk_f[:], in0=psd[pb][:, 0:CW], in1=gt2b[:, csl], op=ALU.mult),
                            reads=[f"psd{pb}", "gtb"], writes=["jkf"])
                        K.op("dve", lambda e, bl=bl, csl=csl: e.tensor_tensor(
                            out=x1[:, bl, csl], in0=x1[:, bl, csl], in1=jk_f[:], op=ALU.add),
                            reads=["jkf", "x1r"], writes=["x1r"])
                for bl in range(4):
                    blk = tt * 4 + bl
                    ob = blk % 2
                    K.op("act", lambda e, bl=bl: e.activation(out=jk[:], in_=x1[:, bl, :], func=AF.Square, accum_out=ss[:, 0:1]),
                         reads=["x1r"], writes=["jkd", "ssd"])
                    K.op("dve", lambda e: e.tensor_scalar(out=ss[:, 1:2], in0=ss[:, 0:1], scalar1=1.0 / D, scalar2=EPS,
                                                         op0=ALU.mult, op1=ALU.add), reads=["ssd"], writes=["ssd"])
                    K.op("act", lambda e: e.activation(out=ss[:, 2:3], in_=ss[:, 1:2], func=AF.Sqrt), reads=["ssd"], writes=["ssd"])
                    K.op("dve", lambda e: e.reciprocal(out=ss[:, 3:4], in_=ss[:, 2:3]), reads=["ssd"], writes=["ssd"])
                    K.op("dve", lambda e, bl=bl, ob=ob: e.scalar_tensor_tensor(
                        out=ot[ob][:], in0=x1[:, bl, :], scalar=ss[:, 3:4], in1=gfb[:], op0=ALU.mult, op1=ALU.mult),
                        reads=["x1r", "ssd", "gfb"], writes=[f"ot{ob}"])
                    K.dma("sp", f"ot{ob}", out[blk * P:(blk + 1) * P, :], ot[ob][:], reads=[f"ot{ob}"], writes=["out"])
        K.finish()
    return nc


_NC_CACHE = {}


def _consts():
    f32 = np.float32
    z = np.arange(ZL)
    pos = np.abs(4095 - z).clip(0, L - 1)
    tlin = np.linspace(0.0, 1.0, L, dtype=f32)
    t = tlin[pos][:, None]
    bands = 16
    w = (f32(2.0 * np.pi) * np.arange(L, dtype=f32) / f32(L))[pos][:, None]
    fr = np.linspace(1e-4, bands - 1, bands, dtype=f32)[None, :]
    zf = np.concatenate([t, np.cos(fr * w), -np.sin(fr * w)], axis=-1).astype(f32)
    BIG = f32(1e4)
    tf = np.where(z <= 4095, t[:, 0], BIG).astype(f32)[None]
    tb = np.where(z > 4095, t[:, 0], BIG).astype(f32)[None]
    max_decay = np.log(1e-2) / 0.3
    min_decay = np.log(1e-2) / 1.5
    deltas = np.linspace(min_decay, max_decay, DHY, dtype=f32)
    negdelta = (-np.abs(deltas)).astype(f32)[None]
    ident = np.eye(P, dtype=f32)
    anti = np.ascontiguousarray(ident[::-1])
    return dict(zT=np.ascontiguousarray(zf.T), tft=tf, tbt=tb, negdelta=negdelta, ident=ident, anti=anti)


def _bias_tables(rpb):
    NEG = np.float32(-30000.0)
    out = np.full((5, 16, P, 5, P), NEG, dtype=np.float32)
    cq = np.arange(64)
    col_start = np.clip(cq - 8, 0, 48)
    for ci, R in enumerate((0, 1, 2, 30, 31)):
        kb0 = min(max(R - 2, 0), 27)
        for qi in range(P):
            r = 2 * R + qi // 64
            q_c = qi % 64
            rs = min(max(r - 4, 0), 56)
            for kbi in range(5):
                for kk in range(P):
                    kr = 2 * (kb0 + kbi) + kk // 64
                    ck = kk % 64
                    if rs <= kr < rs + 8 and col_start[q_c] <= ck < col_start[q_c] + 16:
                        ro = kr - r + 7
                        co = int(np.clip(ck - q_c, -15, 15)) + 15
                        out[ci, :, kk, kbi, qi] = rpb[:, ro, co]
    return out


def kernel(**inp):
    inp = {k: np.asarray(v) for k, v in inp.items()}
    if "nc" not in _NC_CACHE:
        _NC_CACHE["nc"] = build_nc()
    nc = _NC_CACHE["nc"]
    cs = _consts()
    f = lambda a: np.ascontiguousarray(a, dtype=np.float32)
    shared = dict(
        w_ada=f(inp["w_ada"][0]), b_ada=f(inp["b_ada"][0][None]), g_mix=f(inp["g_mix"][0][None]),
        w_in=f(inp["w_in"][0]), tbl=_bias_tables(np.asarray(inp["na_rpb"][0], dtype=np.float32)),
        hy_short_w=f(inp["hy_short_w"][0]), hy_short_b=f(inp["hy_short_b"][0][None]),
        fw1=f(inp["hy_filt_w1"][0]), fb1=f(inp["hy_filt_b1"][0][:, None]),
        fw2=f(inp["hy_filt_w2"][0]), fb2=f(inp["hy_filt_b2"][0][:, None]),
        fw3=f(inp["hy_filt_w3"][0]), fb3=f(inp["hy_filt_b3"][0][:, None]),
        fw4=f(inp["hy_filt_w4"][0]), ffq=f(inp["hy_filt_freq"][0][:, None]),
        hy_bias=f(inp["hy_bias"][0][None]), beta_na=f(inp["beta_na"][0][None]), beta_hy=f(inp["beta_hy"][0][None]),
        w_out=f(inp["w_out"][0]), g_ffn=f(inp["g_ffn"][0][None]), w_up=f(inp["w_up"][0]),
        ffn_conv_w=f(inp["ffn_conv_w"][0]), ffn_conv_b=f(inp["ffn_conv_b"][0][None]),
        w_down=f(inp["w_down"][0]), g_final=f(inp["g_final"][None]), **cs)
    in_maps = []
    for core in range(8):
        b = core // 2
        m = dict(shared)
        m["x"] = f(inp["x"][b])
        m["c"] = f(inp["c"][b][None])
        in_maps.append(m)
    res = run_bass_kernel_spmd(nc, in_maps, core_ids=list(range(8)))
    outp = np.stack([np.asarray(res.results[2 * b]["out"], dtype=np.float32) for b in range(4)], axis=0)
    return outp
```
